# Optimizing a Trainium2 kernel written in Bass

```python
import math
import jax, jax.numpy as jnp
from jax import lax
import numpy as np

D_MODEL = 1024
BATCH = 4
SEQ = 4096
DEPTH = 2

HEAD_DIM = 64
A_Q_HEADS = 8
A_KV_HEADS = 2
WINDOW = 128
B_HEADS = 8
WIN_ROWS = 8
WIN_COLS = 16
QCOLS = 16
KCOLS = QCOLS + WIN_COLS
C_Q_HEADS = 8
C_KV_HEADS = 2

A_Q = A_Q_HEADS * HEAD_DIM
A_KV = A_KV_HEADS * HEAD_DIM
B_W = B_HEADS * HEAD_DIM
C_Q = C_Q_HEADS * HEAD_DIM
C_KV = C_KV_HEADS * HEAD_DIM
BRANCH_WIDTH = A_Q
N_BRANCH = 3
IN_SIZES = (A_Q, A_KV, A_KV, B_W, B_W, B_W, C_Q, C_KV, C_KV, N_BRANCH * D_MODEL)
N_IN = A_Q + 2 * A_KV + 3 * B_W + C_Q + 2 * C_KV + N_BRANCH * D_MODEL

D_FF = 2816
META = 16
BLOCK = 128
PAD_FRONT = BLOCK - META
GRID_W = 64
ROPE_THETA = 10000.0
EPS = 1e-6
NEG = -1e30

kernel_name = "hybrid_gated_window_neighbourhood_axial_encoder"


def rms_norm(x, g):
    xf = x.astype(jnp.float32)
    y = xf * lax.rsqrt(jnp.mean(xf * xf, axis=-1, keepdims=True) + EPS)
    return (y * g.astype(jnp.float32)).astype(x.dtype)


def swiglu(h, w_in, w_out):
    gate, up = jnp.split(h @ w_in, 2, axis=-1)
    return (jax.nn.silu(gate) * up) @ w_out


def rope_angles(pos, dim):
    inv_freq = ROPE_THETA ** (-jnp.arange(0, dim, 2, dtype=jnp.float32) / dim)
    return pos.astype(jnp.float32)[:, None] * inv_freq[None, :]


def apply_rope(x, ang):
    half = x.shape[-1] // 2
    cos = jnp.cos(ang)[None, :, None, :]
    sin = jnp.sin(ang)[None, :, None, :]
    xf = x.astype(jnp.float32)
    x1, x2 = xf[..., :half], xf[..., half:]
    return jnp.concatenate([x1 * cos - x2 * sin, x2 * cos + x1 * sin], axis=-1).astype(x.dtype)


def apply_axial_rope(x, row, col):
    half = x.shape[-1] // 2
    xr = apply_rope(x[..., :half], rope_angles(row, half))
    xc = apply_rope(x[..., half:], rope_angles(col, half))
    return jnp.concatenate([xr, xc], axis=-1)


def window_sink_attention(q, k, v, sink):
    B, L, Hq, dh = q.shape
    Hkv = k.shape[2]
    G = Hq // Hkv
    Lp = L + PAD_FRONT
    nb = Lp // BLOCK
    scale = dh ** -0.5
    qb = jnp.pad(q, ((0, 0), (PAD_FRONT, 0), (0, 0), (0, 0))).reshape(B, nb, BLOCK, Hkv, G, dh)
    kv_pad = ((0, 0), (PAD_FRONT + BLOCK, BLOCK), (0, 0), (0, 0))
    kp = jnp.pad(k, kv_pad).reshape(B, nb + 2, BLOCK, Hkv, dh)
    vp = jnp.pad(v, kv_pad).reshape(B, nb + 2, BLOCK, Hkv, dh)
    kb = jnp.concatenate([kp[:, :-2], kp[:, 1:-1], kp[:, 2:]], axis=2)
    vb = jnp.concatenate([vp[:, :-2], vp[:, 1:-1], vp[:, 2:]], axis=2)
    q_pos = np.arange(nb)[:, None] * BLOCK + np.arange(BLOCK)[None, :]
    k_pos = (np.arange(nb)[:, None] - 1) * BLOCK + np.arange(3 * BLOCK)[None, :]
    kpb = k_pos[:, None, :]
    band = (np.abs(q_pos[:, :, None] - kpb) <= WINDOW) & (kpb >= PAD_FRONT + META) & (kpb < Lp)
    s_win = jnp.einsum('bnqhgd,bnkhd->bnhgqk', qb, kb).astype(jnp.float32) * scale
    s_win = jnp.where(band[None, :, None, None], s_win, NEG)
    km, vm = k[:, :META], v[:, :META]
    s_meta = jnp.einsum('bnqhgd,bmhd->bnhgqm', qb, km).astype(jnp.float32) * scale
    s = jnp.concatenate([s_meta, s_win], axis=-1)
    sk = sink.astype(jnp.float32).reshape(Hkv, G)[None, None, :, :, None, None]
    m = jnp.maximum(jnp.max(s, axis=-1, keepdims=True), sk)
    p = jnp.exp(s - m)
    p = (p / (jnp.sum(p, axis=-1, keepdims=True) + jnp.exp(sk - m))).astype(v.dtype)
    o = (jnp.einsum('bnhgqm,bmhd->bnqhgd', p[..., :META], vm)
         + jnp.einsum('bnhgqk,bnkhd->bnqhgd', p[..., META:], vb))
    return o.reshape(B, Lp, Hq * dh)[:, PAD_FRONT:]


def neighbourhood_attention(q, k, v, rpb):
    B, L, H, dh = q.shape
    S = L - META
    rows = S // GRID_W
    kr = min(WIN_ROWS, rows)
    ncb = GRID_W // QCOLS
    scale = dh ** -0.5
    qm, km, vm = q[:, :META], k[:, :META], v[:, :META]
    qg = q[:, META:].reshape(B, rows, ncb, QCOLS, H, dh)
    kt, vt = k[:, META:], v[:, META:]
    r = np.arange(rows)
    key_rows = np.clip(r - kr // 2, 0, rows - kr)[:, None] + np.arange(kr)[None, :]
    qcol = np.arange(ncb)[:, None] * QCOLS + np.arange(QCOLS)[None, :]
    qcol_start = np.clip(qcol - WIN_COLS // 2, 0, GRID_W - WIN_COLS)
    key_cols = (np.clip(np.arange(ncb) * QCOLS - WIN_COLS // 2, 0, GRID_W - KCOLS)[:, None]
                + np.arange(KCOLS)[None, :])
    tok = (key_rows[:, None, :, None] * GRID_W + key_cols[None, :, None, :]).reshape(rows, ncb, kr * KCOLS)
    kg = kt[:, tok]
    vg = vt[:, tok]
    col_ok = ((key_cols[:, None, :] >= qcol_start[:, :, None])
              & (key_cols[:, None, :] < qcol_start[:, :, None] + WIN_COLS))
    col_ok = np.broadcast_to(col_ok[:, :, None, :], (ncb, QCOLS, kr, KCOLS)).reshape(ncb, QCOLS, kr * KCOLS)
    dr = key_rows - r[:, None] + WIN_ROWS - 1
    dc = np.clip(key_cols[:, None, :] - qcol[:, :, None] + WIN_COLS - 1, 0, 2 * WIN_COLS - 2)
    bias = rpb[:, dr[:, None, None, :, None], dc[None, :, :, None, :]]
    bias = bias.reshape(H, rows, ncb, QCOLS, kr * KCOLS).astype(jnp.float32)
    s_win = jnp.einsum('brcqhd,brckhd->bhrcqk', qg, kg).astype(jnp.float32) * scale + bias[None]
    s_win = jnp.where(col_ok[None, None, None], s_win, NEG)
    s_meta = jnp.einsum('brcqhd,bmhd->bhrcqm', qg, km).astype(jnp.float32) * scale
    p = jax.nn.softmax(jnp.concatenate([s_meta, s_win], axis=-1), axis=-1).astype(v.dtype)
    og = (jnp.einsum('bhrcqm,bmhd->brcqhd', p[..., :META], vm)
          + jnp.einsum('bhrcqk,brckhd->brcqhd', p[..., META:], vg)).reshape(B, S, H * dh)
    sm = jnp.einsum('bmhd,bnhd->bhmn', qm, km).astype(jnp.float32) * scale
    om = jnp.einsum('bhmn,bnhd->bmhd', jax.nn.softmax(sm, axis=-1).astype(v.dtype), vm).reshape(B, META, H * dh)
    return jnp.concatenate([om, og], axis=1)


def dense_block_attention(q, k, v):
    B, L, Hq, dh = q.shape
    Hkv = k.shape[2]
    G = Hq // Hkv
    Lp = L + PAD_FRONT
    nb = Lp // BLOCK
    scale = dh ** -0.5
    pad = ((0, 0), (PAD_FRONT, 0), (0, 0), (0, 0))
    qb = jnp.pad(q, pad).reshape(B, nb, BLOCK, Hkv, G, dh)
    kp = jnp.pad(k, pad)
    vp = jnp.pad(v, pad)
    key_ok = np.arange(Lp) >= PAD_FRONT

    def one_block(qblk):
        s = jnp.einsum('bqhgd,bkhd->bhgqk', qblk, kp).astype(jnp.float32) * scale
        s = jnp.where(key_ok, s, NEG)
        p = jax.nn.softmax(s, axis=-1).astype(vp.dtype)
        return jnp.einsum('bhgqk,bkhd->bqhgd', p, vp)

    o = lax.map(one_block, jnp.moveaxis(qb, 1, 0))
    return jnp.moveaxis(o, 0, 1).reshape(B, Lp, Hq * dh)[:, PAD_FRONT:]


def hybrid_mixer(n, w_in, sink, rpb, qn, kn, w_branch, w_out, pos, row, col):
    B, L, _ = n.shape
    offsets = np.cumsum(IN_SIZES)[:-1].tolist()
    qa, ka, va, qb, kb, vb, qc, kc, vc, gates = jnp.split(n @ w_in, offsets, axis=-1)
    heads = lambda t, h: t.reshape(B, L, h, HEAD_DIM)
    ang = rope_angles(pos, HEAD_DIM)
    oa = window_sink_attention(apply_rope(heads(qa, A_Q_HEADS), ang), apply_rope(heads(ka, A_KV_HEADS), ang),
                               heads(va, A_KV_HEADS), sink)
    ob = neighbourhood_attention(heads(qb, B_HEADS), heads(kb, B_HEADS), heads(vb, B_HEADS), rpb)
    qc = apply_axial_rope(rms_norm(heads(qc, C_Q_HEADS), qn), row, col)
    kc = apply_axial_rope(rms_norm(heads(kc, C_KV_HEADS), kn), row, col)
    oc = dense_block_attention(qc, kc, heads(vc, C_KV_HEADS))
    branches = jnp.stack([oa, ob, oc], axis=2)
    up = jnp.einsum('blnc,ncd->blnd', branches, w_branch)
    g = jax.nn.sigmoid(gates.reshape(B, L, N_BRANCH, D_MODEL))
    return jnp.sum(g * up, axis=2) @ w_out


def setup_inputs(seed: int = 0) -> dict:
    key = jax.random.key(seed)
    ks = jax.random.split(key, 17)
    f32 = jnp.float32

    def nrm(k, shape, scale):
        return jax.random.normal(k, shape, f32) * scale

    def gain(k, shape):
        return 1.0 + 0.02 * jax.random.normal(k, shape, f32)

    return {
        "x": nrm(ks[0], (BATCH, SEQ, D_MODEL), 1.0),
        "meta_tokens": nrm(ks[1], (META, D_MODEL), 1.0),
        "ffn1_norm": gain(ks[2], (DEPTH, D_MODEL)),
        "ffn1_w_in": nrm(ks[3], (DEPTH, D_MODEL, 2 * D_FF), D_MODEL ** -0.5),
        "ffn1_w_out": nrm(ks[4], (DEPTH, D_FF, D_MODEL), D_FF ** -0.5),
        "mix_norm": gain(ks[5], (DEPTH, D_MODEL)),
        "w_in": nrm(ks[6], (DEPTH, D_MODEL, N_IN), D_MODEL ** -0.5),
        "sink_a": nrm(ks[7], (DEPTH, A_Q_HEADS), 0.5),
        "rpb_b": nrm(ks[8], (DEPTH, B_HEADS, 2 * WIN_ROWS - 1, 2 * WIN_COLS - 1), 0.1),
        "qnorm_c": gain(ks[9], (DEPTH, HEAD_DIM)),
        "knorm_c": gain(ks[10], (DEPTH, HEAD_DIM)),
        "w_branch": nrm(ks[11], (DEPTH, N_BRANCH, BRANCH_WIDTH, D_MODEL), BRANCH_WIDTH ** -0.5),
        "w_out": nrm(ks[12], (DEPTH, D_MODEL, D_MODEL), D_MODEL ** -0.5),
        "ffn2_norm": gain(ks[13], (DEPTH, D_MODEL)),
        "ffn2_w_in": nrm(ks[14], (DEPTH, D_MODEL, 2 * D_FF), D_MODEL ** -0.5),
        "ffn2_w_out": nrm(ks[15], (DEPTH, D_FF, D_MODEL), D_FF ** -0.5),
        "final_norm": gain(ks[16], (D_MODEL,)),
    }


def reference(x, meta_tokens, ffn1_norm, ffn1_w_in, ffn1_w_out, mix_norm, w_in, sink_a, rpb_b,
              qnorm_c, knorm_c, w_branch, w_out, ffn2_norm, ffn2_w_in, ffn2_w_out, final_norm):
    B, S, D = x.shape
    L = S + META
    h = jnp.concatenate([jnp.broadcast_to(meta_tokens.astype(x.dtype)[None], (B, META, D)), x], axis=1)
    p = np.arange(L)
    t = p - META
    pos = jnp.asarray(p)
    row = jnp.asarray(np.where(t >= 0, t // GRID_W, -1))
    col = jnp.asarray(np.where(t >= 0, t % GRID_W, p))
    for l in range(DEPTH):
        h = h + 0.5 * swiglu(rms_norm(h, ffn1_norm[l]), ffn1_w_in[l], ffn1_w_out[l])
        h = h + hybrid_mixer(rms_norm(h, mix_norm[l]), w_in[l], sink_a[l], rpb_b[l], qnorm_c[l], knorm_c[l],
                             w_branch[l], w_out[l], pos, row, col)
        h = h + 0.5 * swiglu(rms_norm(h, ffn2_norm[l]), ffn2_w_in[l], ffn2_w_out[l])
    return rms_norm(h, final_norm)[:, META:]
```

```python
import numpy as np
from contextlib import ExitStack
import concourse.bass as bass
import concourse.mybir as mybir
from concourse.bass_utils import run_bass_kernel_spmd

F32, BF16 = mybir.dt.float32, mybir.dt.bfloat16
AF = mybir.ActivationFunctionType
ALU = mybir.AluOpType
P = 128
EPS = 1e-6
NEGB = -1000.0
B_TILES = [(0, list(range(-2, 4))), (1, list(range(-2, 3))), (2, list(range(-2, 3))),
           (3, list(range(-2, 3))), (4, list(range(-3, 3)))]
NBT = sum(len(o) for _, o in B_TILES)


class Cfg:
    def __init__(s, D=1024, DFF=2816, S=4096, L=2, B=4):
        s.D, s.DFF, s.S, s.L, s.B = D, DFF, S, L, B
        s.DC, s.FC = D // P, DFF // P
        s.R = S // 2
        s.NB = s.R // P
        s.NT = s.R + 16
        s.groups = [(i * 512, 512) for i in range(s.R // 512)] + [(s.R, 16)]
        s.sgs = []
        cur, w = [], 0
        for g in s.groups:
            if w + g[1] > 1040:
                s.sgs.append(cur)
                cur, w = [], 0
            cur.append(g)
            w += g[1]
        s.sgs.append(cur)
        s.SGW = max(sum(g[1] for g in sg) for sg in s.sgs)
        s.NQ = 28 + 3 * s.DC
        s.XA = 2 * 128 + 2 * 256
        s.XB = 2 * 1024 + 2 * 1024
        s.XC = s.R + s.NB * 256


import types


def _snap(fn):
    if fn is None or not fn.__closure__:
        return fn
    cells = []
    for c in fn.__closure__:
        try:
            cells.append(types.CellType(c.cell_contents))
        except ValueError:
            cells.append(c)
    return types.FunctionType(fn.__code__, fn.__globals__, fn.__name__, fn.__defaults__, tuple(cells))


class Sched:
    ENG = ("pe", "act", "dve", "pool", "sp")

    def __init__(s):
        s.prog = {e: [] for e in s.ENG}
        s.cnt = {e: 0 for e in s.ENG}
        s.dcum = {}
        s.waited = {}
        s.lastw = {}
        s.readers = {}

    def _deps(s, reads, writes):
        d = {}

        def add(k, v):
            if d.get(k, 0) < v:
                d[k] = v
        for r in reads:
            t = s.lastw.get(r)
            if t:
                add(*t)
        for w in writes:
            t = s.lastw.get(w)
            if t:
                add(*t)
            for k, v in s.readers.get(w, {}).items():
                add(k, v)
        return d

    def _commit(s, tok, reads, writes):
        for r in reads:
            rd = s.readers.setdefault(r, {})
            if rd.get(tok[0], 0) < tok[1]:
                rd[tok[0]] = tok[1]
        for w in writes:
            s.lastw[w] = tok
            s.readers[w] = {}

    def _waits(s, eng, deps, skip_self):
        waits = []
        for k, v in deps.items():
            if skip_self and k == ("e", eng):
                continue
            if s.waited.get((eng, k), 0) < v:
                s.waited[(eng, k)] = v
                waits.append((k, v))
        return waits

    def op(s, eng, fn, reads=(), writes=()):
        waits = s._waits(eng, s._deps(reads, writes), eng == 'pe')
        s.cnt[eng] += 1
        key = ("e", eng)
        s.prog[eng].append((waits, _snap(fn), key, 1))
        s._commit((key, s.cnt[eng]), reads, writes)

    def dma(s, q, sem, fn, reads=(), writes=(), inc=16):
        key = ("d", q, sem)
        deps = s._deps(reads, writes)
        if s.dcum.get(key, 0) > deps.get(key, 0):
            deps[key] = s.dcum[key]
        waits = s._waits(q, deps, False)
        s.dcum[key] = s.dcum.get(key, 0) + inc
        s.prog[q].append((waits, _snap(fn), key, inc))
        s._commit((key, s.dcum[key]), reads, writes)

    def finish(s):
        waits = []
        for k, v in s.dcum.items():
            if s.waited.get(("sp", k), 0) < v:
                waits.append((k, v))
        for e in ("pe", "act", "dve", "pool"):
            if s.cnt[e]:
                waits.append((("e", e), s.cnt[e]))
        s.prog["sp"].append((waits, None, None, 0))

    def emit(s, nc, es):
        sems = {}

        def sem(k):
            if k not in sems:
                sems[k] = es.enter_context(nc.semaphore("s_" + "_".join(str(x) for x in k)))
            return sems[k]
        block = es.enter_context(nc.Block())
        reg = {"pe": block.tensor, "act": block.scalar, "dve": block.vector, "pool": block.gpsimd,
               "sp": block.sync}
        for eng in s.ENG:
            prog = s.prog[eng]

            def body(e, prog=prog):
                for waits, fn, key, inc in prog:
                    for k, v in waits:
                        e.wait_ge(sem(k), v)
                    if fn is not None:
                        ins = fn(e)
                        ins.then_inc(sem(key), inc)
            reg[eng](body)


def _rope_tables(cfg, half):
    R, NT = cfg.R, cfg.NT
    t = np.arange(R) + half * R
    posA = np.concatenate([16 + t, np.arange(16)]).astype(np.float32)
    row = np.concatenate([t // 64, -np.ones(16)]).astype(np.float32)
    col = np.concatenate([t % 64, np.arange(16)]).astype(np.float32)
    theta = np.float32(10000.0)
    invA = (theta ** (-np.arange(0, 64, 2, dtype=np.float32) / np.float32(64))).astype(np.float32)
    invC = (theta ** (-np.arange(0, 32, 2, dtype=np.float32) / np.float32(32))).astype(np.float32)
    ropeA = np.zeros((P, 2, NT), np.float32)
    ropeC = np.zeros((P, 2, NT), np.float32)
    for p in range(P):
        d = p % 64
        ang = posA * invA[d % 32]
        ropeA[p, 0] = np.cos(ang)
        ropeA[p, 1] = np.sin(ang) * (-1.0 if d < 32 else 1.0)
        dd = d % 32
        ang = (row if d < 32 else col) * invC[dd % 16]
        ropeC[p, 0] = np.cos(ang)
        ropeC[p, 1] = np.sin(ang) * (-1.0 if dd < 16 else 1.0)
    return ropeA, ropeC


def _swapA(d):
    return (d + 32) % 64


def _swapC(d):
    return (d // 32) * 32 + ((d % 32) + 16) % 32


def _maskA(cfg, half):
    a = np.arange(P)[:, None]
    b = np.arange(P)[None, :]
    lo = (b <= a).astype(np.float32)
    hi = (a <= b).astype(np.float32)
    one = np.ones((P, P), np.float32)
    m = np.zeros((P, 10, P), np.float32)
    for ty in range(3):
        left_ok = not (ty == 0 and half == 0)
        right_ok = not (ty == 2 and half == 1)
        m[:, ty * 3 + 0] = lo if left_ok else 0
        m[:, ty * 3 + 1] = one
        m[:, ty * 3 + 2] = hi if right_ok else 0
    m[:, 9, :16] = (np.arange(P)[:, None] <= 112 + np.arange(16)[None, :]).astype(np.float32)
    return m


def _biasB(cfg, half, rpb_l):
    NB = cfg.NB
    rows = cfg.S // 64
    kr = min(8, rows)
    out = np.full((8, P, NBT, P), NEGB, np.float32)
    reps = [0, 1, 2, NB - 2, NB - 1]
    kc = np.arange(64)
    ti = 0
    for ty, offs in B_TILES:
        gblk = half * NB + reps[ty]
        for o in offs:
            kblk = gblk + o
            for qp in range(2):
                r = 2 * gblk + qp
                kstart = min(max(r - kr // 2, 0), rows - kr)
                for kp in range(2):
                    krow = 2 * kblk + kp
                    if kblk < 0 or kblk >= 2 * NB or krow < kstart or krow >= kstart + kr:
                        continue
                    dr = krow - r + 7
                    for qc in range(64):
                        qcs = min(max(qc - 8, 0), 64 - 16)
                        ok = (kc >= qcs) & (kc < qcs + 16)
                        dc = np.clip(kc - qc + 15, 0, 30)
                        vals = rpb_l[:, dr, dc]
                        blk = out[:, kp * 64:(kp + 1) * 64, ti, qp * 64 + qc]
                        blk[:, ok] = vals[:, ok]
            ti += 1
    return out


def prep_inputs(cfg, inp):
    L, D, DFF, DC, FC, R, NT = cfg.L, cfg.D, cfg.DFF, cfg.DC, cfg.FC, cfg.R, cfg.NT
    f = lambda a: np.ascontiguousarray(np.asarray(a, dtype=np.float32))
    x, meta = f(inp["x"]), f(inp["meta_tokens"])
    gn = np.zeros((P, 3 * L + 1, DC), np.float32)
    for l in range(L):
        gn[:, 3 * l + 0] = f(inp["ffn1_norm"])[l].reshape(DC, P).T
        gn[:, 3 * l + 1] = f(inp["mix_norm"])[l].reshape(DC, P).T
        gn[:, 3 * l + 2] = f(inp["ffn2_norm"])[l].reshape(DC, P).T
    gn[:, 3 * L] = f(inp["final_norm"]).reshape(DC, P).T
    w1 = np.zeros((2 * L, FC, P, DC, 256), np.float32)
    w2 = np.zeros((2 * L, DC, P, FC, 128), np.float32)
    for l in range(L):
        for fi, (ki, ko) in enumerate((("ffn1_w_in", "ffn1_w_out"), ("ffn2_w_in", "ffn2_w_out"))):
            wi = f(inp[ki])[l].reshape(DC, P, 2, FC, 128)
            w1[2 * l + fi] = wi.transpose(3, 1, 0, 2, 4).reshape(FC, P, DC, 256)
            wo_ = f(inp[ko])[l].reshape(FC, P, DC, 128)
            w2[2 * l + fi] = wo_.transpose(2, 1, 0, 3)
    oqa, oka, ova, oqb, okb, ovb, oqc, okc, ovc, og = 0, 512, 640, 768, 1280, 1792, 2304, 2816, 2944, 3072
    d64 = np.arange(64)
    cols = []

    def gqa_chunks(off, sw):
        for c in range(4):
            cols.append(np.concatenate([off + c * 64 + sw(d64), off + (c + 4) * 64 + sw(d64)]))
    ident = lambda d: d
    gqa_chunks(oqa, ident); gqa_chunks(oqa, _swapA)
    cols.append(np.concatenate([oka + d64, oka + 64 + d64])); cols.append(np.concatenate([oka + _swapA(d64), oka + 64 + _swapA(d64)]))
    for c in range(4):
        cols.append(oqb + c * 128 + np.arange(128))
    for c in range(4):
        cols.append(okb + c * 128 + np.arange(128))
    gqa_chunks(oqc, ident); gqa_chunks(oqc, _swapC)
    cols.append(np.concatenate([okc + d64, okc + 64 + d64])); cols.append(np.concatenate([okc + _swapC(d64), okc + 64 + _swapC(d64)]))
    for c in range(3 * DC):
        cols.append(og + c * 128 + np.arange(128))
    cols = np.stack(cols)
    win = f(inp["w_in"])
    wq = np.zeros((L, cfg.NQ, P, DC, 128), np.float32)
    wva = np.zeros((L, P, DC, 128), np.float32)
    wvb = np.zeros((L, P, DC, 512), np.float32)
    wvc = np.zeros((L, P, DC, 128), np.float32)
    for l in range(L):
        wl = win[l].reshape(DC, P, -1)
        wq[l] = wl[:, :, cols].transpose(2, 1, 0, 3)
        wva[l] = wl[:, :, ova:ova + 128].transpose(1, 0, 2)
        wvb[l] = wl[:, :, ovb:ovb + 512].transpose(1, 0, 2)
        wvc[l] = wl[:, :, ovc:ovc + 128].transpose(1, 0, 2)
    pidx = np.arange(P)
    wbr = np.zeros((L, 3, P, 4, D), np.float32)
    wbf = f(inp["w_branch"])
    for n in range(3):
        for cc in range(4):
            if n == 1:
                ch = cc * 128 + pidx
            else:
                ch = (cc + 4 * (pidx // 64)) * 64 + pidx % 64
            wbr[:, n, :, cc, :] = wbf[:, n, ch, :]
    wo = f(inp["w_out"]).reshape(L, DC, P, D).transpose(0, 2, 1, 3)
    sink = f(inp["sink_a"])
    sinkT = np.zeros((P, L, 4), np.float32)
    for c in range(4):
        sinkT[:64, :, c] = sink[:, c + 4][None, :]
        sinkT[64:, :, c] = sink[:, c][None, :]
    qn, kn = f(inp["qnorm_c"]), f(inp["knorm_c"])
    dd = pidx % 64
    gc = np.zeros((P, L, 4), np.float32)
    for l in range(L):
        gc[:, l, 0] = qn[l][dd]; gc[:, l, 1] = qn[l][_swapC(dd)]
        gc[:, l, 2] = kn[l][dd]; gc[:, l, 3] = kn[l][_swapC(dd)]
    rpb = f(inp["rpb_b"])
    shared = dict(gn=gn, w1=w1, w2=w2, wq=wq, wva=wva, wvb=wvb, wvc=wvc, wbr=wbr, wo=np.ascontiguousarray(wo),
                  sinkT=sinkT, gc=gc)
    per_half = []
    for half in range(2):
        ra, rc = _rope_tables(cfg, half)
        bb = np.stack([_biasB(cfg, half, rpb[l]) for l in range(L)])
        per_half.append(dict(ropeA=ra, ropeC=rc, maskA=_maskA(cfg, half), biasB=bb))
    maps = []
    for b in range(cfg.B):
        for half in range(2):
            h0 = np.concatenate([x[b, half * R:(half + 1) * R], meta], 0)
            xT = np.ascontiguousarray(h0.T.reshape(DC, P, NT).transpose(1, 0, 2))
            m = dict(shared)
            m.update(per_half[half])
            m["xT"] = xT
            maps.append(m)
    return maps


def build(cfg):
    L, D, DFF, DC, FC, R, NB, NT, NQ = cfg.L, cfg.D, cfg.DFF, cfg.DC, cfg.FC, cfg.R, cfg.NB, cfg.NT, cfg.NQ
    nc = bass.Bass("TRN2", target_bir_lowering=False)
    S = Sched()
    es = ExitStack()

    def din(name, shape):
        return nc.dram_tensor(name, list(shape), F32, kind="ExternalInput").ap()
    xT = din("xT", [P, DC, NT]); gn_d = din("gn", [P, 3 * L + 1, DC])
    w1_d = din("w1", [2 * L, FC, P, DC, 256]); w2_d = din("w2", [2 * L, DC, P, FC, 128])
    wq_d = din("wq", [L, NQ, P, DC, 128])
    wv_d = {"A": din("wva", [L, P, DC, 128]), "B": din("wvb", [L, P, DC, 512]), "C": din("wvc", [L, P, DC, 128])}
    wbr_d = din("wbr", [L, 3, P, 4, D]); wo_d = din("wo", [L, P, DC, D])
    sink_d = din("sinkT", [P, L, 4]); gc_d = din("gc", [P, L, 4])
    rope_d = {"A": din("ropeA", [P, 2, NT]), "C": din("ropeC", [P, 2, NT])}
    maskA_d = din("maskA", [P, 10, P]); biasB_d = din("biasB", [L, 8, P, NBT, P])
    outT = nc.dram_tensor("outT", [P, DC, R], F32, kind="ExternalOutput").ap()
    hD = nc.dram_tensor("hD", [P, DC, NT], F32).ap()
    oD = nc.dram_tensor("oD", [3, P, 4, NT], BF16).ap()
    XW = {"A": cfg.XA, "B": cfg.XB, "C": cfg.XC}
    kvloc = {b: nc.dram_tensor("kvloc" + b, [P, XW[b]], BF16) for b in "ABC"}
    kvall = {b: nc.dram_tensor("kvall" + b, [2 * P, XW[b]], BF16) for b in "ABC"}

    def sb(name, shape, dt):
        return es.enter_context(nc.sbuf_tensor(name, list(shape), dt))
    gnS = sb("gnS", [P, 3 * L + 1, DC], F32); sinkS = sb("sinkS", [P, L, 4], F32); gcS = sb("gcS", [P, L, 4], F32)
    esink = sb("esink", [P, L, 4], F32)
    maskS = sb("maskS", [P, 10, P], BF16)
    ones = sb("ones", [P, P], BF16); bones = sb("bones", [P, P], BF16)
    scr = sb("scr", [P, 8], F32)
    nTall = sb("nTall", [P, DC, NT], BF16)
    sq = sb("sq", [P, DC, 512], BF16)
    tmpf = [sb("tmpf%d" % i, [P, 512], F32) for i in range(7)]
    psum = es.enter_context(nc.psum_tensor("psum", [P, 4096], F32))
    bank = lambda i: psum[:, i * 512:(i + 1) * 512]

    AW = 72 * 1024
    arena = sb("arena", [P, AW], BF16)
    apos = [0]

    def carve(shape, dt):
        n = int(np.prod(shape[1:]))
        w = n * (2 if dt == F32 else 1)
        a0 = (apos[0] + 1) // 2 * 2
        apos[0] = a0 + w
        assert apos[0] <= AW, ("arena overflow", apos[0])
        v = arena[:, a0:a0 + w]
        if dt == F32:
            v = v.bitcast(F32)
        if len(shape) == 3:
            v = v.rearrange("p (a b) -> p a b", a=shape[1])
        elif len(shape) == 4:
            v = v.rearrange("p (a b c) -> p a b c", a=shape[1], b=shape[2])
        return v

    cnt = {"ps": 0}

    def barrier():
        S.op("pool", lambda e: e.memset(scr[:, 0:1], 0.0), reads=(), writes=("ARENA",))

    AR = ("ARENA",)

    def mm(e, out, lhsT, rhs, start, stop):
        return e.matmul(out, lhsT, rhs, start=start, stop=stop)

    def mmgroup(out, pairs, reads, writes):
        def fn(e):
            ins = None
            n = len(pairs)
            for i, (a, b) in enumerate(pairs):
                ins = mm(e, out, a, b, i == 0, i == n - 1)
            return ins
        S.op("pe", fn, reads=tuple(reads) + AR, writes=writes)

    S.dma("sp", "c0", lambda e: e.dma_start(out=gnS[:], in_=gn_d), writes=("gnS",))
    S.dma("sp", "c1", lambda e: e.dma_start(out=sinkS[:], in_=sink_d), writes=("sinkS",))
    S.dma("sp", "c2", lambda e: e.dma_start(out=gcS[:], in_=gc_d), writes=("gcS",))
    S.dma("pool", "c3", lambda e: e.dma_start(out=maskS[:], in_=maskA_d), writes=("maskS",))
    S.op("pool", lambda e: e.memset(ones[:], 1.0), writes=("ones",))
    S.op("pool", lambda e: e.memset(bones[:], 0.0), writes=("bones",))
    S.op("pool", lambda e: e.memset(bones[0:64, 0:64], 1.0), writes=("bones",), reads=())
    S.op("pool", lambda e: e.memset(bones[64:128, 64:128], 1.0), writes=("bones",))
    S.op("act", lambda e: e.activation(out=esink[:], in_=sinkS[:], func=AF.Exp), reads=("sinkS",), writes=("esink",))

    def rmsnorm(hb, hkey, ni, out, okey, gs, gain_ap=None):
        S.op("act", lambda e: e.activation(out=sq[:, :, :gs], in_=hb, func=AF.Square),
             reads=(hkey,) + AR, writes=("sq",))
        pb = bank(0)
        mmgroup(pb[:, :gs], [(ones[:], sq[:, c, :gs]) for c in range(DC)], ("ones", "sq"), (("ps", 0),))
        t0, t1 = tmpf[0], tmpf[1]
        S.op("dve", lambda e: e.tensor_scalar(out=t0[:, :gs], in0=pb[:, :gs], scalar1=1.0 / D, scalar2=EPS,
                                              op0=ALU.mult, op1=ALU.add), reads=(("ps", 0),) + AR, writes=("t0",))
        S.op("act", lambda e: e.activation(out=t1[:, :gs], in_=t0[:, :gs], func=AF.Sqrt), reads=("t0",), writes=("t1",))
        S.op("dve", lambda e: e.reciprocal(out=t0[:, :gs], in_=t1[:, :gs]), reads=("t1",), writes=("t0",))
        for c in range(DC):
            S.op("dve", lambda e, c=c: e.scalar_tensor_tensor(out=out[:, c, :gs], in0=hb[:, c, :gs],
                                                               scalar=gnS[:, ni, c:c + 1], in1=t0[:, :gs],
                                                               op0=ALU.mult, op1=ALU.mult),
                 reads=(hkey, "t0", "gnS") + AR, writes=(okey,))

    def hsrc_ap(l, f):
        return xT if (l == 0 and f == 0) else hD

    def ffn(l, f, final):
        ni = 3 * l + (0 if f == 0 else 2)
        wi = 2 * l + f
        apos[0] = 0
        hsg = carve([P, DC, cfg.SGW], F32)
        nTs = carve([P, DC, cfg.SGW], BF16)
        aT = carve([P, FC, cfg.SGW], BF16)
        w1b = [carve([P, DC, 256], BF16) for _ in range(3)]
        w2b = [carve([P, FC, 128], BF16) for _ in range(2)]
        sil = [carve([P, 512], F32) for _ in range(2)]
        ost = carve([P, DC, 512], F32)
        src = hsrc_ap(l, f)
        for si, sg in enumerate(cfg.sgs):
            offs = []
            so = 0
            for gi, (g0, gs) in enumerate(sg):
                offs.append(so)
                hk = ("hsg", gi)
                S.dma("sp", "h%d" % gi, lambda e, so=so, g0=g0, gs=gs: e.dma_start(out=hsg[:, :, so:so + gs], in_=src[:, :, g0:g0 + gs]),
                      reads=(("hD", g0),) + AR, writes=(hk,))
                rmsnorm(hsg[:, :, so:so + gs], hk, ni, nTs[:, :, so:so + gs], ("nTs", gi), gs)
                so += gs
            for j in range(FC):
                sl = j % 3
                S.dma("pool", "w1_%d" % sl, lambda e, sl=sl, j=j: e.dma_start(out=w1b[sl][:], in_=w1_d[wi, j]),
                      reads=AR, writes=(("w1b", sl),))
                for gi, (g0, gs) in enumerate(sg):
                    so = offs[gi]
                    k2 = cnt["ps"] % 2
                    cnt["ps"] += 1
                    pg, pu = bank(1 + k2), bank(3 + k2)
                    mmgroup(pg[:, :gs], [(w1b[sl][:, k, 0:128], nTs[:, k, so:so + gs]) for k in range(DC)],
                            (("w1b", sl), ("nTs", gi)), (("ps", 1 + k2),))
                    mmgroup(pu[:, :gs], [(w1b[sl][:, k, 128:256], nTs[:, k, so:so + gs]) for k in range(DC)],
                            (("w1b", sl), ("nTs", gi)), (("ps", 3 + k2),))
                    S.op("act", lambda e, pg=pg, k2=k2, gs=gs: e.activation(out=sil[k2][:, :gs], in_=pg[:, :gs], func=AF.Silu),
                         reads=(("ps", 1 + k2),) + AR, writes=(("sil", k2),))
                    S.op("dve", lambda e, pu=pu, k2=k2, gs=gs, so=so, j=j: e.tensor_tensor(
                        out=aT[:, j, so:so + gs], in0=pu[:, :gs], in1=sil[k2][:, :gs], op=ALU.mult),
                        reads=(("ps", 3 + k2), ("sil", k2)) + AR, writes=(("aT", j, gi),))
            for c in range(DC):
                sl = c % 2
                S.dma("pool", "w2_%d" % sl, lambda e, sl=sl, c=c: e.dma_start(out=w2b[sl][:], in_=w2_d[wi, c]),
                      reads=AR, writes=(("w2b", sl),))
                for gi, (g0, gs) in enumerate(sg):
                    so = offs[gi]
                    k2 = cnt["ps"] % 2
                    cnt["ps"] += 1
                    po = bank(5 + k2)
                    mmgroup(po[:, :gs], [(w2b[sl][:, j, :], aT[:, j, so:so + gs]) for j in range(FC)],
                            [("w2b", sl)] + [("aT", j, gi) for j in range(FC)], (("ps", 5 + k2),))
                    S.op("dve", lambda e, po=po, c=c, so=so, gs=gs: e.scalar_tensor_tensor(
                        out=hsg[:, c, so:so + gs], in0=po[:, :gs], scalar=0.5, in1=hsg[:, c, so:so + gs],
                        op0=ALU.mult, op1=ALU.add), reads=(("ps", 5 + k2), ("hsg", gi)) + AR, writes=(("hsg", gi),))
            for gi, (g0, gs) in enumerate(sg):
                so = offs[gi]
                if not final:
                    S.dma("sp", "hs%d" % gi, lambda e, so=so, g0=g0, gs=gs: e.dma_start(out=hD[:, :, g0:g0 + gs], in_=hsg[:, :, so:so + gs]),
                          reads=(("hsg", gi),) + AR, writes=(("hD", g0),))
                elif g0 < R:
                    rmsnorm(hsg[:, :, so:so + gs], ("hsg", gi), 3 * L, ost[:, :, :gs], "ost", gs)
                    S.dma("sp", "hs%d" % gi, lambda e, g0=g0, gs=gs: e.dma_start(out=outT[:, :, g0:g0 + gs], in_=ost[:, :, :gs]),
                          reads=("ost",) + AR, writes=(("outT", g0),))
        barrier()

    def mixer(l):
        ni = 3 * l + 1
        apos[0] = 0
        hb = carve([P, DC, 512], F32)
        qo = carve([P, 4, NT], BF16)
        wqb = [carve([P, DC, 128], BF16) for _ in range(4)]
        wvb_ = carve([P, DC, 512], BF16)
        ropeS = carve([P, 2, NT], F32)
        btf = carve([P, NBT, P], F32)
        btE = carve([P, NBT, P], BF16)
        PT = [carve([P, 8, P], BF16) for _ in range(2)]
        rs = carve([P, 512], F32)
        rs2 = carve([P, 512], F32)
        kvbase = apos[0]
        for (g0, gs) in cfg.groups:
            S.dma("sp", "hb", lambda e, g0=g0, gs=gs: e.dma_start(out=hb[:, :, :gs], in_=hD[:, :, g0:g0 + gs]),
                  reads=(("hD", g0),) + AR, writes=("hb",))
            rmsnorm(hb[:, :, :gs], "hb", ni, nTall[:, :, g0:g0 + gs], ("nT", g0), gs)
        nTkeys = tuple(("nT", g0) for g0, _ in cfg.groups)
        wcnt = [0]

        def load_wq(ci):
            sl = wcnt[0] % 4
            wcnt[0] += 1
            S.dma("pool", "wq%d" % sl, lambda e: e.dma_start(out=wqb[sl][:], in_=wq_d[l, ci]), reads=AR, writes=(("wqb", sl),))
            return sl

        def proj(sl, g0, gs, bk):
            pb = bank(bk)
            mmgroup(pb[:, :gs], [(wqb[sl][:, k, :], nTall[:, k, g0:g0 + gs]) for k in range(DC)],
                    (("wqb", sl), ("nT", g0)), (("ps", bk),))
            return pb

        def rope_store(dst, dkey, p0, p1, gs, g0, cnorm=None):
            t1, t2, t3 = tmpf[2], tmpf[3], tmpf[4]
            cos, sin = ropeS[:, 0, g0:g0 + gs], ropeS[:, 1, g0:g0 + gs]
            rd = (("ps", 1), ("ps", 3), "ropeS") + AR
            if cnorm is None:
                S.op("dve", lambda e: e.tensor_tensor(out=t1[:, :gs], in0=p0[:, :gs], in1=cos, op=ALU.mult), reads=rd, writes=("t1r",))
                S.op("dve", lambda e: e.tensor_tensor(out=t2[:, :gs], in0=p1[:, :gs], in1=sin, op=ALU.mult), reads=rd, writes=("t2r",))
                S.op("pool", lambda e: e.tensor_tensor(out=dst, in0=t1[:, :gs], in1=t2[:, :gs], op=ALU.add),
                     reads=("t1r", "t2r") + AR, writes=(dkey,))
            else:
                gi0, gi1 = cnorm
                S.op("act", lambda e: e.activation(out=sq[:, 0, :gs], in_=p0[:, :gs], func=AF.Square), reads=rd, writes=("sq",))
                pb = bank(7)
                mmgroup(pb[:, :gs], [(bones[:], sq[:, 0, :gs])], ("bones", "sq"), (("ps", 7),))
                S.op("dve", lambda e: e.tensor_scalar(out=rs[:, :gs], in0=pb[:, :gs], scalar1=1.0 / 64, scalar2=EPS,
                                                      op0=ALU.mult, op1=ALU.add), reads=(("ps", 7),) + AR, writes=("rs",))
                S.op("act", lambda e: e.activation(out=rs2[:, :gs], in_=rs[:, :gs], func=AF.Sqrt), reads=("rs",) + AR, writes=("rs2",))
                S.op("dve", lambda e: e.reciprocal(out=rs[:, :gs], in_=rs2[:, :gs]), reads=("rs2",) + AR, writes=("rs",))
                S.op("dve", lambda e: e.scalar_tensor_tensor(out=t1[:, :gs], in0=p0[:, :gs], scalar=gcS[:, l, gi0:gi0 + 1], in1=cos,
                                                             op0=ALU.mult, op1=ALU.mult), reads=rd + ("gcS",), writes=("t1r",))
                S.op("dve", lambda e: e.scalar_tensor_tensor(out=t2[:, :gs], in0=p1[:, :gs], scalar=gcS[:, l, gi1:gi1 + 1], in1=sin,
                                                             op0=ALU.mult, op1=ALU.mult), reads=rd + ("gcS",), writes=("t2r",))
                S.op("pool", lambda e: e.tensor_tensor(out=t3[:, :gs], in0=t1[:, :gs], in1=t2[:, :gs], op=ALU.add),
                     reads=("t1r", "t2r") + AR, writes=("t3r",))
                S.op("dve", lambda e: e.tensor_tensor(out=dst, in0=t3[:, :gs], in1=rs[:, :gs], op=ALU.mult),
                     reads=("t3r", "rs") + AR, writes=(dkey,))

        def normalize(po, sr, pr, dst, dkey, n, addsink=None):
            ta, tb = tmpf[5], tmpf[6]
            if addsink is not None:
                S.op("dve", lambda e: e.tensor_scalar(out=ta[sr, :n], in0=po[sr, :n], scalar1=addsink, scalar2=None, op0=ALU.add),
                     reads=(("ps", po_key[0]), "esink") + AR, writes=("nta",))
                S.op("dve", lambda e: e.reciprocal(out=ta[sr, :n], in_=ta[sr, :n]), reads=("nta",), writes=("nta",))
            else:
                S.op("dve", lambda e: e.reciprocal(out=ta[sr, :n], in_=po[sr, :n]), reads=(("ps", po_key[0]),) + AR, writes=("nta",))
            S.op("dve", lambda e: e.tensor_copy(out=tb[pr, :n], in_=ta[sr, :n]), reads=("nta",), writes=("ntb",))
            S.op("dve", lambda e: e.tensor_tensor(out=dst, in0=po[pr, :n], in1=tb[pr, :n], op=ALU.mult),
                 reads=(("ps", po_key[0]), "ntb") + AR, writes=(dkey,))
        po_key = [5]

        def exchange(b, stores):
            kl, ka = kvloc[b], kvall[b]
            for i, (c0, w, ap_, key) in enumerate(stores):
                keys = key if (isinstance(key, tuple) and key and isinstance(key[0], tuple)) else (key,)
                S.dma("sp", "x%d" % (i % 4), lambda e, c0=c0, w=w, ap_=ap_: e.dma_start(out=kl.ap()[:, c0:c0 + w], in_=ap_),
                      reads=tuple(keys) + AR, writes=(("kvloc", b),))
            S.dma("pool", "cc", lambda e: e.collective_compute("AllGather", ALU.bypass,
                                                                 replica_groups=[[2 * i, 2 * i + 1] for i in range(cfg.B)],
                                                                 ins=[kl.ap().opt()], outs=[ka.ap().opt()]),
                  reads=(("kvloc", b),) + AR, writes=(("kvall", b),), inc=1)

        def xload(b, i, dst, rank, c0, w, key):
            ka = kvall[b].ap()
            src = ka[rank * P:(rank + 1) * P, c0:c0 + w]
            S.dma("sp", "xl%d" % (i % 4), lambda e: e.dma_start(out=dst, in_=src), reads=(("kvall", b),) + AR, writes=(key,))

        def store_o(bi):
            S.dma("sp", "ost", lambda e: e.dma_start(out=oD[bi], in_=qo[:]),
                  reads=tuple(("qo", c, g0) for c in range(4) for g0, _ in cfg.groups) + AR, writes=(("oD", bi),))

        def vproj(b, VW, sink_fn):
            S.dma("pool", "wv", lambda e: e.dma_start(out=wvb_[:, :, :VW], in_=wv_d[b][l]), reads=AR, writes=("wvb",))
            tiles = [(t * P, P) for t in range(NB)] + [(R, 16)]
            for ti, (t0, nt) in enumerate(tiles):
                g0 = [g for g, gs in cfg.groups if g <= t0 < g + gs][0]
                bk = 1 + (ti % 2) * 2
                pb = bank(bk)
                mmgroup(pb[:nt, :VW], [(nTall[:, k, t0:t0 + nt], wvb_[:, k, :VW]) for k in range(DC)],
                        ("wvb", ("nT", g0)), (("ps", bk),))
                sink_fn(ti, nt, pb, bk)

        apos[0] = kvbase
        KW = (NB + 3) * P + 16
        kaT = carve([P, KW], BF16)
        vaa = carve([P, NB + 4, 2, P], BF16)
        S.op("pool", lambda e: e.memset(vaa[:], 1.0), reads=AR, writes=("vaa",))
        S.dma("sp", "rope", lambda e: e.dma_start(out=ropeS[:], in_=rope_d["A"]), reads=AR, writes=("ropeS",))
        for c in range(5):
            s0 = load_wq(c if c < 4 else 8)
            s1 = load_wq(4 + c if c < 4 else 9)
            for (g0, gs) in cfg.groups:
                p0 = proj(s0, g0, gs, 1)
                p1 = proj(s1, g0, gs, 3)
                if c < 4:
                    rope_store(qo[:, c, g0:g0 + gs], ("qo", c, g0), p0, p1, gs, g0)
                else:
                    kc0 = P + g0 if g0 < R else (NB + 3) * P
                    rope_store(kaT[:, kc0:kc0 + gs], ("kaT", g0), p0, p1, gs, g0)

        def va_sink(ti, nt, pb, bk):
            blk = ti + 1 if ti < NB else NB + 3
            for g in range(2):
                S.op("act", lambda e, g=g: e.copy(out=vaa[:nt, blk, g, g * 64:(g + 1) * 64], in_=pb[:nt, g * 64:(g + 1) * 64]),
                     reads=(("ps", bk), "vaa") + AR, writes=(("vaa", blk),))
        vproj("A", 128, va_sink)
        kkeys = tuple(("kaT", g0) for g0, _ in cfg.groups)
        exchange("A", [(0, P, kaT[:, P:2 * P], ("kaT", 0)), (P, P, kaT[:, NB * P:(NB + 1) * P], ("kaT", cfg.groups[-2][0])),
                       (2 * P, 256, vaa[:, 1].rearrange("p a b -> p (a b)"), ("vaa", 1)),
                       (2 * P + 256, 256, vaa[:, NB].rearrange("p a b -> p (a b)"), ("vaa", NB))])
        xload("A", 0, kaT[:, 0:P], 0, P, P, ("kaT", "L"))
        xload("A", 1, kaT[:, (NB + 1) * P:(NB + 2) * P], 1, 0, P, ("kaT", "R"))
        xload("A", 2, kaT[:, (NB + 2) * P:(NB + 3) * P], 0, 0, P, ("kaT", "H"))
        xload("A", 3, vaa[:, 0].rearrange("p a b -> p (a b)"), 0, 2 * P + 256, 256, ("vaa", 0))
        xload("A", 4, vaa[:, NB + 1].rearrange("p a b -> p (a b)"), 1, 2 * P, 256, ("vaa", NB + 1))
        xload("A", 5, vaa[:, NB + 2].rearrange("p a b -> p (a b)"), 0, 2 * P, 256, ("vaa", NB + 2))
        kall = kkeys + (("kaT", "L"), ("kaT", "R"), ("kaT", "H"))
        vall = tuple(("vaa", i) for i in range(NB + 4)) + ("vaa",)
        it = 0
        for c in range(4):
            for g in range(2):
                pr, sr = slice(g * 64, g * 64 + 64), slice((1 - g) * 64, (1 - g) * 64 + 64)
                for j in range(NB + 1):
                    k2 = it % 2
                    it += 1
                    pst, pt = bank(1 + 2 * k2), PT[k2]
                    po = bank(5 + k2)
                    po_key[0] = 5 + k2
                    if j < NB:
                        q0, nq = j * P, P
                        eb = [j, j + 1, j + 2]
                        ty = 0 if j == 0 else (2 if j == NB - 1 else 1)
                        msk = maskS[:, ty * 3:ty * 3 + 3, :]
                    else:
                        q0, nq = R, 16
                        eb = [NB + 2]
                        msk = maskS[:, 9:10, :16]
                    qkey = ("qo", c, [g0 for g0, gs in cfg.groups if g0 <= q0 < g0 + gs][0])
                    nk = len(eb)

                    def stf(e, eb=eb, q0=q0, nq=nq, pst=pst):
                        ins = None
                        for s_, eblk in enumerate(eb):
                            ins = mm(e, pst[:, s_ * nq:(s_ + 1) * nq], kaT[pr, eblk * P:(eblk + 1) * P], qo[pr, c, q0:q0 + nq], True, True)
                        ins = mm(e, pst[:16, 3 * P:3 * P + nq], kaT[pr, (NB + 3) * P:(NB + 3) * P + 16], qo[pr, c, q0:q0 + nq], True, True)
                        return ins
                    S.op("pe", stf, reads=kall + (qkey,) + AR, writes=(("ps", 1 + 2 * k2),))
                    S.op("act", lambda e, pst=pst, pt=pt, nk=nk, nq=nq: e.activation(
                        out=pt[:, 0:nk, :nq], in_=pst[:, :nk * nq].rearrange("p (a b) -> p a b", a=nk), func=AF.Exp, scale=0.125),
                        reads=(("ps", 1 + 2 * k2),) + AR, writes=(("PT", k2),))
                    S.op("act", lambda e, pst=pst, pt=pt, nq=nq: e.activation(
                        out=pt[:16, 3, :nq], in_=pst[:16, 3 * P:3 * P + nq], func=AF.Exp, scale=0.125),
                        reads=(("ps", 1 + 2 * k2),) + AR, writes=(("PT", k2),))
                    S.op("dve", lambda e, pt=pt, nk=nk, nq=nq, msk=msk: e.tensor_tensor(
                        out=pt[:, 0:nk, :nq], in0=pt[:, 0:nk, :nq], in1=msk, op=ALU.mult),
                        reads=(("PT", k2), "maskS") + AR, writes=(("PT", k2),))

                    def pvf(e, eb=eb, nq=nq, pt=pt, po=po):
                        mm(e, po[:, :nq], vaa[:16, NB + 3, g, :], pt[:16, 3, :nq], True, False)
                        ins = None
                        for s_, eblk in enumerate(eb):
                            ins = mm(e, po[:, :nq], vaa[:, eblk, g, :], pt[:, s_, :nq], False, s_ == len(eb) - 1)
                        return ins
                    S.op("pe", pvf, reads=vall + (("PT", k2),) + AR, writes=(("ps", 5 + k2),))
                    normalize(po, sr, pr, qo[pr, c, q0:q0 + nq], qkey, nq, addsink=esink[sr, l, c:c + 1])
        store_o(0)
        barrier()

        apos[0] = kvbase
        KWB = (NB + 4) * P + 16
        kbT = carve([P, 4, KWB], BF16)
        vbe = carve([P, NB + 5, 512], BF16)
        for c in range(8):
            s0 = load_wq(10 + c)
            for (g0, gs) in cfg.groups:
                p0 = proj(s0, g0, gs, 1 + 2 * (c % 2))
                bk = 1 + 2 * (c % 2)
                if c < 4:
                    S.op("act", lambda e, p0=p0, c=c, g0=g0, gs=gs: e.copy(out=qo[:, c, g0:g0 + gs], in_=p0[:, :gs]),
                         reads=(("ps", bk),) + AR, writes=(("qo", c, g0),))
                else:
                    kc0 = 2 * P + g0 if g0 < R else (NB + 4) * P
                    S.op("act", lambda e, p0=p0, c=c, kc0=kc0, gs=gs: e.copy(out=kbT[:, c - 4, kc0:kc0 + gs], in_=p0[:, :gs]),
                         reads=(("ps", bk),) + AR, writes=(("kbT", g0),))

        def vb_sink(ti, nt, pb, bk):
            blk = ti + 2 if ti < NB else NB + 4
            S.op("act", lambda e: e.copy(out=vbe[:nt, blk, :], in_=pb[:nt, :512]), reads=(("ps", bk),) + AR, writes=(("vbe", blk),))
        vproj("B", 512, vb_sink)
        gl = cfg.groups[-2][0]
        st = []
        for c in range(4):
            st.append((c * 256, 256, kbT[:, c, 2 * P:4 * P], ("kbT", 0)))
            st.append((1024 + c * 256, 256, kbT[:, c, NB * P:(NB + 2) * P], ("kbT", gl)))
        for i in range(2):
            st.append((2048 + i * 512, 512, vbe[:, 2 + i, :], ("vbe", 2 + i)))
            st.append((3072 + i * 512, 512, vbe[:, NB + i, :], ("vbe", NB + i)))
        exchange("B", st)
        for c in range(4):
            xload("B", c, kbT[:, c, 0:2 * P], 0, 1024 + c * 256, 256, ("kbT", "L"))
            xload("B", c + 1, kbT[:, c, (NB + 2) * P:(NB + 4) * P], 1, c * 256, 256, ("kbT", "R"))
        for i in range(2):
            xload("B", i, vbe[:, i, :], 0, 3072 + i * 512, 512, ("vbe", i))
            xload("B", i + 2, vbe[:, NB + 2 + i, :], 1, 2048 + i * 512, 512, ("vbe", NB + 2 + i))
        kall = tuple(("kbT", g0) for g0, _ in cfg.groups) + (("kbT", "L"), ("kbT", "R"))
        vall = tuple(("vbe", i) for i in range(NB + 5))
        tstart = {}
        ti_ = 0
        for ty, offs in B_TILES:
            tstart[ty] = ti_
            ti_ += len(offs)
        it = 0
        for h in range(8):
            c, g = h // 2, h % 2
            pr = slice(g * 64, g * 64 + 64)
            S.dma("sp", "btf", lambda e, h=h: e.dma_start(out=btf[:], in_=biasB_d[l, h]), reads=AR, writes=("btf",))
            S.op("act", lambda e: e.activation(out=btE[:], in_=btf[:], func=AF.Exp), reads=("btf",) + AR, writes=("btE",))
            for j in range(NB + 1):
                k2 = it % 2
                it += 1
                pt = PT[k2]
                pa, pbk = bank(1 + 2 * k2), bank(2 + 2 * k2)
                po = bank(5 + k2)
                if j < NB:
                    q0, nq = j * P, P
                    ty = 0 if j == 0 else 1 if j == 1 else 3 if j == NB - 2 else 4 if j == NB - 1 else 2
                    offs = B_TILES[ty][1]
                    eb = [j + 2 + o for o in offs]
                else:
                    q0, nq, eb, offs = R, 16, [], []
                nk = len(eb)
                qkey = ("qo", c, [g0 for g0, gs in cfg.groups if g0 <= q0 < g0 + gs][0])

                def stf(e, eb=eb, q0=q0, nq=nq, pa=pa, pbk=pbk):
                    for s_, eblk in enumerate(eb):
                        dst = pa[:, s_ * P:(s_ + 1) * P] if s_ < 4 else pbk[:, (s_ - 4) * P:(s_ - 3) * P]
                        mm(e, dst, kbT[pr, c, eblk * P:(eblk + 1) * P], qo[pr, c, q0:q0 + nq], True, True)
                    return mm(e, pbk[:16, 3 * P:3 * P + nq], kbT[pr, c, (NB + 4) * P:(NB + 4) * P + 16], qo[pr, c, q0:q0 + nq], True, True)
                S.op("pe", stf, reads=kall + (qkey,) + AR, writes=(("ps", 1 + 2 * k2), ("ps", 2 + 2 * k2)))
                if nk:
                    n1 = min(nk, 4)
                    S.op("act", lambda e, pa=pa, pt=pt, n1=n1: e.activation(
                        out=pt[:, 0:n1, :], in_=pa[:, :n1 * P].rearrange("p (a b) -> p a b", a=n1), func=AF.Exp, scale=0.125),
                        reads=(("ps", 1 + 2 * k2),) + AR, writes=(("PT", k2),))
                    if nk > 4:
                        S.op("act", lambda e, pbk=pbk, pt=pt, nk=nk: e.activation(
                            out=pt[:, 4:nk, :], in_=pbk[:, :(nk - 4) * P].rearrange("p (a b) -> p a b", a=nk - 4), func=AF.Exp, scale=0.125),
                            reads=(("ps", 2 + 2 * k2),) + AR, writes=(("PT", k2),))
                    t0_ = tstart[ty]
                    S.op("dve", lambda e, pt=pt, nk=nk, t0_=t0_: e.tensor_tensor(
                        out=pt[:, 0:nk, :], in0=pt[:, 0:nk, :], in1=btE[:, t0_:t0_ + nk, :], op=ALU.mult),
                        reads=(("PT", k2), "btE") + AR, writes=(("PT", k2),))
                S.op("act", lambda e, pbk=pbk, pt=pt, nq=nq: e.activation(
                    out=pt[:16, 7, :nq], in_=pbk[:16, 3 * P:3 * P + nq], func=AF.Exp, scale=0.125),
                    reads=(("ps", 2 + 2 * k2),) + AR, writes=(("PT", k2),))

                def pvf(e, eb=eb, nq=nq, pt=pt, po=po):
                    mm(e, po[:, :nq], vbe[:16, NB + 4, c * P:(c + 1) * P], pt[:16, 7, :nq], True, nk == 0)
                    for s_, eblk in enumerate(eb):
                        mm(e, po[:, :nq], vbe[:, eblk, c * P:(c + 1) * P], pt[:, s_, :nq], False, s_ == nk - 1)
                    ins = mm(e, po[:, P:P + nq], ones[:16, :], pt[:16, 7, :nq], True, nk == 0)
                    for s_, eblk in enumerate(eb):
                        ins = mm(e, po[:, P:P + nq], ones[:], pt[:, s_, :nq], False, s_ == nk - 1)
                    return ins
                S.op("pe", pvf, reads=vall + (("PT", k2), "ones") + AR, writes=(("ps", 5 + k2),))
                ta = tmpf[5]
                S.op("dve", lambda e, po=po, nq=nq: e.reciprocal(out=ta[pr, :nq], in_=po[pr, P:P + nq]),
                     reads=(("ps", 5 + k2),) + AR, writes=("nta",))
                S.op("dve", lambda e, po=po, nq=nq, q0=q0: e.tensor_tensor(out=qo[pr, c, q0:q0 + nq], in0=po[pr, :nq], in1=ta[pr, :nq], op=ALU.mult),
                     reads=(("ps", 5 + k2), "nta") + AR, writes=(qkey,))
        store_o(1)
        barrier()

        apos[0] = kvbase
        kcT = carve([P, 2 * R + 16], BF16)
        vca = carve([P, 2 * NB + 1, 2, P], BF16)
        kcl = carve([P, R], BF16)
        vcl = carve([P, NB, 2, P], BF16)
        S.op("pool", lambda e: e.memset(vca[:], 1.0), reads=AR, writes=("vca",))
        S.op("pool", lambda e: e.memset(vcl[:], 1.0), reads=AR, writes=("vcl",))
        S.dma("sp", "rope", lambda e: e.dma_start(out=ropeS[:], in_=rope_d["C"]), reads=AR, writes=("ropeS",))
        for c in range(5):
            s0 = load_wq(18 + c if c < 4 else 26)
            s1 = load_wq(22 + c if c < 4 else 27)
            for (g0, gs) in cfg.groups:
                p0 = proj(s0, g0, gs, 1)
                p1 = proj(s1, g0, gs, 3)
                if c < 4:
                    rope_store(qo[:, c, g0:g0 + gs], ("qo", c, g0), p0, p1, gs, g0, cnorm=(0, 1))
                elif g0 < R:
                    rope_store(kcl[:, g0:g0 + gs], ("kcl", g0), p0, p1, gs, g0, cnorm=(2, 3))
                else:
                    rope_store(kcT[:, 2 * R:2 * R + 16], ("kcT", "M"), p0, p1, gs, g0, cnorm=(2, 3))

        def vc_sink(ti, nt, pb, bk):
            for g in range(2):
                if ti < NB:
                    S.op("act", lambda e, g=g: e.copy(out=vcl[:nt, ti, g, g * 64:(g + 1) * 64], in_=pb[:nt, g * 64:(g + 1) * 64]),
                         reads=(("ps", bk), "vcl") + AR, writes=(("vcl", ti),))
                else:
                    S.op("act", lambda e, g=g: e.copy(out=vca[:nt, 2 * NB, g, g * 64:(g + 1) * 64], in_=pb[:nt, g * 64:(g + 1) * 64]),
                         reads=(("ps", bk), "vca") + AR, writes=(("vca", "M"),))
        vproj("C", 128, vc_sink)
        exchange("C", [(0, R, kcl[:], tuple(("kcl", g0) for g0, gs in cfg.groups if g0 < R)),
                       (R, NB * 256, vcl[:].rearrange("p a b c -> p (a b c)"), tuple(("vcl", ti) for ti in range(NB)))])
        for rk in range(2):
            xload("C", rk, kcT[:, rk * R:(rk + 1) * R], rk, 0, R, ("kcT", rk))
            xload("C", 2 + rk, vca[:, rk * NB:(rk + 1) * NB].rearrange("p a b c -> p (a b c)"), rk, R, NB * 256, ("vca", rk))
        kall = (("kcT", 0), ("kcT", 1), ("kcT", "M"))
        vall = (("vca", 0), ("vca", 1), ("vca", "M"), "vca")
        it = 0
        og = 0
        for c in range(4):
            for g in range(2):
                pr, sr = slice(g * 64, g * 64 + 64), slice((1 - g) * 64, (1 - g) * 64 + 64)
                for (g0, gs) in cfg.groups:
                    ok2 = og % 2
                    og += 1
                    po = bank(5 + ok2)
                    po_key[0] = 5 + ok2
                    qkey = ("qo", c, g0)
                    for kb in range(2 * NB + 1):
                        k2 = it % 2
                        it += 1
                        pst, pt = bank(1 + 2 * k2), PT[k2]
                        if kb == 0:
                            nk_, kap, vap = 16, kcT[pr, 2 * R:2 * R + 16], vca[:16, 2 * NB, g, :]
                        else:
                            nk_, kap, vap = P, kcT[pr, (kb - 1) * P:kb * P], vca[:, kb - 1, g, :]
                        ptv = pt[:nk_].rearrange("p a b -> p (a b)")[:, :gs]
                        S.op("pe", lambda e, pst=pst, kap=kap, nk_=nk_: mm(e, pst[:nk_, :gs], kap, qo[pr, c, g0:g0 + gs], True, True),
                             reads=kall + (qkey,) + AR, writes=(("ps", 1 + 2 * k2),))
                        S.op("act", lambda e, pst=pst, ptv=ptv, nk_=nk_: e.activation(out=ptv, in_=pst[:nk_, :gs], func=AF.Exp, scale=0.125),
                             reads=(("ps", 1 + 2 * k2),) + AR, writes=(("PT", k2),))
                        S.op("pe", lambda e, po=po, vap=vap, ptv=ptv, kb=kb: mm(e, po[:, :gs], vap, ptv, kb == 0, kb == 2 * NB),
                             reads=vall + (("PT", k2),) + AR, writes=(("ps", 5 + ok2),))
                    normalize(po, sr, pr, qo[pr, c, g0:g0 + gs], qkey, gs)
        store_o(2)
        barrier()

        apos[0] = 0
        hb = carve([P, DC, 512], F32)
        gW = carve([P, 3 * DC, DC, P], BF16)
        bW = carve([P, 3, 4, D], BF16)
        oW = carve([P, DC, D], BF16)
        ob = carve([P, 3, 4, 512], BF16)
        mT = carve([P, DC, 512], BF16)
        sg_ = [carve([P, 512], F32) for _ in range(3)]
        tt = [carve([P, 512], F32) for _ in range(3)]
        for i in range(3 * DC):
            S.dma("pool", "gw%d" % (i % 4), lambda e, i=i: e.dma_start(out=gW[:, i], in_=wq_d[l, 28 + i]), reads=AR, writes=("gW",))
        for n in range(3):
            S.dma("pool", "bw%d" % n, lambda e, n=n: e.dma_start(out=bW[:, n], in_=wbr_d[l, n]), reads=AR, writes=("bW",))
        S.dma("pool", "ow", lambda e: e.dma_start(out=oW[:], in_=wo_d[l]), reads=AR, writes=("oW",))
        for (g0, gs) in cfg.groups:
            S.dma("sp", "hb", lambda e, g0=g0, gs=gs: e.dma_start(out=hb[:, :, :gs], in_=hD[:, :, g0:g0 + gs]),
                  reads=(("hD", g0),) + AR, writes=("hb",))
            for n in range(3):
                S.dma("sp", "ob%d" % n, lambda e, n=n, g0=g0, gs=gs: e.dma_start(out=ob[:, n, :, :gs], in_=oD[n, :, :, g0:g0 + gs]),
                      reads=(("oD", n),) + AR, writes=(("ob", n),))
            for c in range(DC):
                for n in range(3):
                    mmgroup(bank(1 + n)[:, :gs], [(gW[:, n * DC + c, k, :], nTall[:, k, g0:g0 + gs]) for k in range(DC)],
                            ("gW", ("nT", g0)), (("ps", 1 + n),))
                    mmgroup(bank(4 + n)[:, :gs], [(bW[:, n, cc, c * P:(c + 1) * P], ob[:, n, cc, :gs]) for cc in range(4)],
                            ("bW", ("ob", n)), (("ps", 4 + n),))
                    S.op("act", lambda e, n=n: e.activation(out=sg_[n][:, :gs], in_=bank(1 + n)[:, :gs], func=AF.Sigmoid),
                         reads=(("ps", 1 + n),) + AR, writes=(("sg", n),))
                    S.op("dve", lambda e, n=n: e.tensor_tensor(out=tt[n][:, :gs], in0=bank(4 + n)[:, :gs], in1=sg_[n][:, :gs], op=ALU.mult),
                         reads=(("ps", 4 + n), ("sg", n)) + AR, writes=(("tt", n),))
                S.op("pool", lambda e: e.tensor_tensor(out=tt[0][:, :gs], in0=tt[0][:, :gs], in1=tt[1][:, :gs], op=ALU.add),
                     reads=(("tt", 0), ("tt", 1)) + AR, writes=(("tt", 0),))
                S.op("pool", lambda e, c=c: e.tensor_tensor(out=mT[:, c, :gs], in0=tt[0][:, :gs], in1=tt[2][:, :gs], op=ALU.add),
                     reads=(("tt", 0), ("tt", 2)) + AR, writes=(("mT", c),))
            for c in range(DC):
                k2 = c % 2
                mmgroup(bank(7 - k2 * 7)[:, :gs], [(oW[:, k, c * P:(c + 1) * P], mT[:, k, :gs]) for k in range(DC)],
                        ("oW",) + tuple(("mT", k) for k in range(DC)), (("ps", 7 - k2 * 7),))
                S.op("dve", lambda e, c=c, k2=k2: e.tensor_tensor(out=hb[:, c, :gs], in0=bank(7 - k2 * 7)[:, :gs], in1=hb[:, c, :gs], op=ALU.add),
                     reads=(("ps", 7 - k2 * 7), "hb") + AR, writes=("hb",))
            S.dma("sp", "hbs", lambda e, g0=g0, gs=gs: e.dma_start(out=hD[:, :, g0:g0 + gs], in_=hb[:, :, :gs]),
                  reads=("hb",) + AR, writes=(("hD", g0),))
        barrier()

    barrier()
    for l in range(L):
        ffn(l, 0, False)
        mixer(l)
        ffn(l, 1, l == L - 1)
    S.finish()
    S.emit(nc, es)
    es.close()
    return nc


_CACHE = {}


def kernel(**inputs):
    cfg = Cfg()
    maps = prep_inputs(cfg, inputs)
    if "nc" not in _CACHE:
        _CACHE["nc"] = build(cfg)
    res = run_bass_kernel_spmd(_CACHE["nc"], maps, core_ids=list(range(8)))
    out = np.zeros((cfg.B, cfg.S, cfg.D), np.float32)
    for b in range(cfg.B):
        for half in range(2):
            oT = np.asarray(res.results[2 * b + half]["outT"], dtype=np.float32)
            out[b, half * cfg.R:(half + 1) * cfg.R] = oT.transpose(2, 1, 0).reshape(cfg.R, cfg.D)
    return out
```
